# Optimizing a Trainium2 kernel written in Bass

```python
import math
import jax
import jax.numpy as jnp
from jax import lax
import numpy as np


D_MODEL = 1024
BATCH = 4
SEQ = 8192
DEPTH = 2

GRID_W = 64
CTX_LEN = 256
EPS = 1e-6
ROPE_BASE = 10000.0
N_MOD = 6
NA_HEADS = 8
NA_HEAD_DIM = D_MODEL // 16
NA_WIDTH = NA_HEADS * NA_HEAD_DIM
NA_WIN_H = 8
NA_WIN_W = 16
FNET_GROUPS = 4
FNET_GROUP_DIM = D_MODEL // 8
FNET_WIDTH = FNET_GROUPS * FNET_GROUP_DIM
EVEN_IN = 3 * NA_WIDTH + FNET_WIDTH
EVEN_MIX = NA_WIDTH + FNET_WIDTH
DIFF_HEADS = 8
DIFF_HEAD_DIM = D_MODEL // 16
DIFF_QK_WIDTH = DIFF_HEADS * 2 * DIFF_HEAD_DIM
DIFF_V_WIDTH = DIFF_HEADS * 2 * DIFF_HEAD_DIM
ODD_IN = 2 * DIFF_QK_WIDTH + DIFF_V_WIDTH
Q_BLOCK = 128
PEER_HEADS = 8
PEER_N_KEYS = 128
PEER_N_EXPERTS = PEER_N_KEYS * PEER_N_KEYS
PEER_TOPK = 16
PEER_KEY_HALF = D_MODEL // 8
PEER_CHUNK = 128

kernel_name = 'hybrid_natten_fnet_diffattn_peer_dit'


def rms_norm(x, g):
    xf = x.astype(jnp.float32)
    y = xf * lax.rsqrt(jnp.mean(xf * xf, axis=-1, keepdims=True) + EPS)
    return (y * g.astype(jnp.float32)).astype(x.dtype)


def modulate(x, g, shift, scale):
    return rms_norm(x, g) * (1 + scale) + shift


def split_heads(t, n_heads):
    b, n, w = t.shape
    return t.reshape(b, n, n_heads, w // n_heads).transpose(0, 2, 1, 3)


def merge_heads(t):
    b, h, n, dh = t.shape
    return t.transpose(0, 2, 1, 3).reshape(b, n, h * dh)


def diff_qk_heads(t):
    b, n, _ = t.shape
    return t.reshape(b, n, DIFF_HEADS, 2, DIFF_HEAD_DIM).transpose(0, 2, 3, 1, 4)


def axial_rope_tables(n_tokens):
    quarter = DIFF_HEAD_DIM // 4
    t = jnp.arange(n_tokens)
    row = (t // GRID_W).astype(jnp.float32)
    col = (t % GRID_W).astype(jnp.float32)
    inv = ROPE_BASE ** (-jnp.arange(quarter, dtype=jnp.float32) / quarter)
    ang = jnp.stack([row[:, None] * inv, col[:, None] * inv], axis=1)
    return jnp.cos(ang), jnp.sin(ang)


def apply_axial_rope(x, cos, sin):
    shp = x.shape
    xf = x.astype(jnp.float32).reshape(shp[:-1] + (2, 2, shp[-1] // 4))
    x1 = xf[..., 0, :]
    x2 = xf[..., 1, :]
    out = jnp.stack([x1 * cos - x2 * sin, x2 * cos + x1 * sin], axis=-2)
    return out.reshape(shp).astype(x.dtype)


def dense_attention(q, k, v):
    s = jnp.einsum('bhqd,bhkd->bhqk', q, k).astype(jnp.float32) * (q.shape[-1] ** -0.5)
    p = jax.nn.softmax(s, axis=-1).astype(v.dtype)
    return jnp.einsum('bhqk,bhkd->bhqd', p, v)


def neighbourhood_attention(q, k, v, k_ctx, v_ctx, rpb):
    b, h, s, dh = q.shape
    rows = s // GRID_W
    win_h = min(NA_WIN_H, rows)
    n_nb = win_h * NA_WIN_W
    scale = dh ** -0.5
    qg = q.reshape(b, h, rows, GRID_W, dh)
    kg = k.reshape(b, h, rows, GRID_W, dh)
    vg = v.reshape(b, h, rows, GRID_W, dh)
    col = np.arange(GRID_W)
    col_start = np.clip(col - NA_WIN_W // 2, 0, GRID_W - NA_WIN_W)
    col_idx = col_start[:, None] + np.arange(NA_WIN_W)[None, :]
    col_bias = rpb[:, :, col_idx - col[:, None] + NA_WIN_W - 1]

    def row_block(r):
        r_start = jnp.clip(r - win_h // 2, 0, rows - win_h)
        q_r = lax.dynamic_index_in_dim(qg, r, axis=2, keepdims=False)
        k_nb = lax.dynamic_slice_in_dim(kg, r_start, win_h, axis=2)[:, :, :, col_idx]
        v_nb = lax.dynamic_slice_in_dim(vg, r_start, win_h, axis=2)[:, :, :, col_idx]
        row_off = r_start + jnp.arange(win_h) - r + NA_WIN_H - 1
        bias = jnp.take(col_bias, row_off, axis=1).transpose(0, 2, 1, 3)
        s_nb = jnp.einsum('bhqd,bhiqkd->bhqik', q_r, k_nb).astype(jnp.float32) * scale + bias[None]
        s_cx = jnp.einsum('bhqd,bhcd->bhqc', q_r, k_ctx).astype(jnp.float32) * scale
        p = jax.nn.softmax(jnp.concatenate([s_nb.reshape(b, h, GRID_W, n_nb), s_cx], axis=-1), axis=-1).astype(v.dtype)
        p_nb = p[..., :n_nb].reshape(b, h, GRID_W, win_h, NA_WIN_W)
        return (jnp.einsum('bhqik,bhiqkd->bhqd', p_nb, v_nb)
                + jnp.einsum('bhqc,bhcd->bhqd', p[..., n_nb:], v_ctx))

    out = lax.map(row_block, jnp.arange(rows))
    return out.transpose(1, 2, 0, 3, 4).reshape(b, h, s, dh)


def fourier_mix(f):
    b, n, _ = f.shape
    fg = f.astype(jnp.float32).reshape(b, n, FNET_GROUPS, FNET_GROUP_DIM)
    y = jnp.fft.fft2(fg, axes=(1, 3), norm='ortho').real
    return y.reshape(b, n, FNET_WIDTH).astype(f.dtype)


def diff_attend(qb, k_all, v_all, lam):
    s = jnp.einsum('bhcqd,bhckd->bhcqk', qb, k_all).astype(jnp.float32) * (qb.shape[-1] ** -0.5)
    p = jax.nn.softmax(s, axis=-1)
    a = p[:, :, 0] - lam * p[:, :, 1]
    return jnp.einsum('bhqk,bhkd->bhqd', a.astype(v_all.dtype), v_all)


def even_mixer(hl, hc, w_in, w_out, rpb, ctx_out):
    cuts = [NA_WIDTH, 2 * NA_WIDTH, 3 * NA_WIDTH]
    ql, kl, vl, fl = jnp.split(hl @ w_in, cuts, axis=-1)
    if ctx_out:
        qc, kc, vc, fc = jnp.split(hc @ w_in, cuts, axis=-1)
    else:
        kc, vc = jnp.split(hc @ w_in[:, NA_WIDTH:3 * NA_WIDTH], 2, axis=-1)
    kc_h = split_heads(kc, NA_HEADS)
    vc_h = split_heads(vc, NA_HEADS)
    a_l = neighbourhood_attention(split_heads(ql, NA_HEADS), split_heads(kl, NA_HEADS),
                                  split_heads(vl, NA_HEADS), kc_h, vc_h, rpb)
    yl = jnp.concatenate([merge_heads(a_l), fourier_mix(fl)], axis=-1) @ w_out
    yc = None
    if ctx_out:
        a_c = dense_attention(split_heads(qc, NA_HEADS), kc_h, vc_h)
        yc = jnp.concatenate([merge_heads(a_c), fourier_mix(fc)], axis=-1) @ w_out
    return yl, yc


def odd_mixer(hl, hc, w_in, w_out, lq1, lk1, lq2, lk2, subln_g, lambda_init, cos, sin, ctx_out):
    b, s, _ = hl.shape
    ql, kl, vl = jnp.split(hl @ w_in, [DIFF_QK_WIDTH, 2 * DIFF_QK_WIDTH], axis=-1)
    kc, vc = jnp.split(hc @ w_in[:, DIFF_QK_WIDTH:], [DIFF_QK_WIDTH], axis=-1)
    f32 = jnp.float32
    lam = (jnp.exp(jnp.sum(lq1.astype(f32) * lk1.astype(f32)))
           - jnp.exp(jnp.sum(lq2.astype(f32) * lk2.astype(f32))) + lambda_init)
    ql = apply_axial_rope(diff_qk_heads(ql), cos, sin)
    kl = apply_axial_rope(diff_qk_heads(kl), cos, sin)
    kc = diff_qk_heads(kc)
    vc_h = split_heads(vc, DIFF_HEADS)
    k_all = jnp.concatenate([kl, kc], axis=3)
    v_all = jnp.concatenate([split_heads(vl, DIFF_HEADS), vc_h], axis=2)
    nblk = s // Q_BLOCK
    qb = ql.reshape(b, DIFF_HEADS, 2, nblk, Q_BLOCK, DIFF_HEAD_DIM).transpose(3, 0, 1, 2, 4, 5)
    o_l = lax.map(lambda q: diff_attend(q, k_all, v_all, lam), qb)
    o_l = o_l.transpose(1, 2, 0, 3, 4).reshape(b, DIFF_HEADS, s, 2 * DIFF_HEAD_DIM)

    def post(o):
        return merge_heads(rms_norm(o, subln_g) * (1.0 - lambda_init))

    yl = post(o_l) @ w_out
    yc = None
    if ctx_out:
        qc = diff_qk_heads(hc @ w_in[:, :DIFF_QK_WIDTH])
        yc = post(diff_attend(qc, kc, vc_h, lam)) @ w_out
    return yl, yc


def peer_ffn(h, w_q, sub_keys, u, v):
    b, n, d = h.shape
    tok = h.reshape((b * n) // PEER_CHUNK, PEER_CHUNK, d)

    def chunk(t):
        q = (t @ w_q).reshape(PEER_CHUNK, PEER_HEADS, 2, PEER_KEY_HALF)
        s = jnp.einsum('thpk,hpnk->thpn', q, sub_keys).astype(jnp.float32)
        s_top, i_top = lax.top_k(s, PEER_TOPK)
        cand_s = s_top[:, :, 0, :, None] + s_top[:, :, 1, None, :]
        cand_i = i_top[:, :, 0, :, None] * PEER_N_KEYS + i_top[:, :, 1, None, :]
        best_s, best_j = lax.top_k(cand_s.reshape(PEER_CHUNK, PEER_HEADS, PEER_TOPK * PEER_TOPK), PEER_TOPK)
        idx = jnp.take_along_axis(cand_i.reshape(PEER_CHUNK, PEER_HEADS, PEER_TOPK * PEER_TOPK), best_j, axis=-1)
        g = jax.nn.softmax(best_s, axis=-1)
        a = jnp.einsum('td,thkd->thk', t, u[idx])
        w = (jax.nn.gelu(a.astype(jnp.float32)) * g).astype(t.dtype)
        return jnp.einsum('thk,thkd->td', w, v[idx])

    return lax.map(chunk, tok).reshape(b, n, d)


def setup_inputs(seed: int = 0) -> dict:
    key = jax.random.key(seed)
    ks = jax.random.split(key, 24)
    d = D_MODEL
    n_even = (DEPTH + 1) // 2
    n_odd = DEPTH // 2

    def nrm(k, shape, scale):
        return jax.random.normal(k, shape, jnp.float32) * scale

    return {
        'x': nrm(ks[0], (BATCH, SEQ, d), 1.0),
        'c': nrm(ks[1], (BATCH, d), 1.0),
        'ctx': nrm(ks[2], (BATCH, CTX_LEN, d), 1.0),
        'c_ctx': nrm(ks[3], (d,), 1.0),
        'w_mod': nrm(ks[4], (DEPTH, d, N_MOD * d), 0.5 * d ** -0.5),
        'b_mod': nrm(ks[5], (DEPTH, N_MOD * d), 0.02),
        'norm_mix_g': 1.0 + nrm(ks[6], (DEPTH, d), 0.02),
        'norm_ffn_g': 1.0 + nrm(ks[7], (DEPTH, d), 0.02),
        'even_w_in': nrm(ks[8], (n_even, d, EVEN_IN), d ** -0.5),
        'even_w_out': nrm(ks[9], (n_even, EVEN_MIX, d), EVEN_MIX ** -0.5),
        'na_rpb': nrm(ks[10], (n_even, NA_HEADS, 2 * NA_WIN_H - 1, 2 * NA_WIN_W - 1), 0.1),
        'odd_w_in': nrm(ks[11], (n_odd, d, ODD_IN), d ** -0.5),
        'odd_w_out': nrm(ks[12], (n_odd, DIFF_V_WIDTH, d), DIFF_V_WIDTH ** -0.5),
        'diff_lambda_q1': nrm(ks[13], (n_odd, DIFF_HEAD_DIM), 0.1),
        'diff_lambda_k1': nrm(ks[14], (n_odd, DIFF_HEAD_DIM), 0.1),
        'diff_lambda_q2': nrm(ks[15], (n_odd, DIFF_HEAD_DIM), 0.1),
        'diff_lambda_k2': nrm(ks[16], (n_odd, DIFF_HEAD_DIM), 0.1),
        'diff_subln_g': 1.0 + nrm(ks[17], (n_odd, 2 * DIFF_HEAD_DIM), 0.02),
        'peer_w_q': nrm(ks[18], (DEPTH, d, PEER_HEADS * 2 * PEER_KEY_HALF), d ** -0.5),
        'peer_sub_keys': nrm(ks[19], (DEPTH, PEER_HEADS, 2, PEER_N_KEYS, PEER_KEY_HALF), PEER_KEY_HALF ** -0.5),
        'peer_u': nrm(ks[20], (DEPTH, PEER_N_EXPERTS, d), d ** -0.5),
        'peer_v': nrm(ks[21], (DEPTH, PEER_N_EXPERTS, d), 0.25),
        'final_norm_g': 1.0 + nrm(ks[22], (d,), 0.02),
    }


def reference(x, c, ctx, c_ctx, w_mod, b_mod, norm_mix_g, norm_ffn_g, even_w_in, even_w_out, na_rpb,
              odd_w_in, odd_w_out, diff_lambda_q1, diff_lambda_k1, diff_lambda_q2, diff_lambda_k2,
              diff_subln_g, peer_w_q, peer_sub_keys, peer_u, peer_v, final_norm_g):
    s = x.shape[1]
    cos, sin = axial_rope_tables(s)
    xl, xc = x, ctx
    for layer in range(DEPTH):
        last = layer == DEPTH - 1
        j = layer // 2
        mod_l = (jax.nn.silu(c) @ w_mod[layer] + b_mod[layer])[:, None, :]
        mod_c = (jax.nn.silu(c_ctx) @ w_mod[layer] + b_mod[layer])[None, None, :]
        sh_m, sc_m, g_m, sh_f, sc_f, g_f = jnp.split(mod_l, N_MOD, axis=-1)
        csh_m, csc_m, cg_m, csh_f, csc_f, cg_f = jnp.split(mod_c, N_MOD, axis=-1)
        hl = modulate(xl, norm_mix_g[layer], sh_m, sc_m)
        hc = modulate(xc, norm_mix_g[layer], csh_m, csc_m)
        if layer % 2 == 0:
            yl, yc = even_mixer(hl, hc, even_w_in[j], even_w_out[j], na_rpb[j], not last)
        else:
            lambda_init = 0.8 - 0.6 * math.exp(-0.3 * layer)
            yl, yc = odd_mixer(hl, hc, odd_w_in[j], odd_w_out[j], diff_lambda_q1[j], diff_lambda_k1[j],
                               diff_lambda_q2[j], diff_lambda_k2[j], diff_subln_g[j], lambda_init,
                               cos, sin, not last)
        xl = xl + g_m * yl
        hl = modulate(xl, norm_ffn_g[layer], sh_f, sc_f)
        xl = xl + g_f * peer_ffn(hl, peer_w_q[layer], peer_sub_keys[layer], peer_u[layer], peer_v[layer])
        if not last:
            xc = xc + cg_m * yc
            hc = modulate(xc, norm_ffn_g[layer], csh_f, csc_f)
            xc = xc + cg_f * peer_ffn(hc, peer_w_q[layer], peer_sub_keys[layer], peer_u[layer], peer_v[layer])
    return rms_norm(xl, final_norm_g)
```

```python
import contextlib
import math
import numpy as np
import concourse.bass as bass
import concourse.mybir as mybir
from concourse.bass_utils import run_bass_kernel_spmd

F32 = mybir.dt.float32
BF16 = mybir.dt.bfloat16
AF = mybir.ActivationFunctionType
ALU = mybir.AluOpType
AX = mybir.AxisListType

D = 1024
SEQ = 8192
NCTX = 256
HALF = 4096
NEXT = 38
NEG = -30000.0
BIG = 1.0e30
GELU = AF.Gelu_apprx_tanh
DEBUG_OUT = set()
PRE_STOP = 99


class Buf:
    def __init__(self, name, t):
        self.name = name
        self.t = t
        self.w = None
        self.r = {}
        self.ds = None
        self.psum = False

    def __getitem__(self, idx):
        return self.t[idx]


class KB:
    def __init__(self, nc):
        self.nc = nc
        self.es = contextlib.ExitStack()
        self.eng = {"pe": nc.tensor, "act": nc.scalar, "dve": nc.vector, "pool": nc.gpsimd, "sp": nc.sync}
        self.sem = {}
        self.cnt = {}
        self.seen = {e: {} for e in self.eng}
        for e in self.eng:
            self.sem[e] = self.es.enter_context(nc.semaphore("sem_" + e))
            self.cnt[e] = 0
        self.dpool = []
        self.dall = []
        self.phase_stack = None
        self.phase_bufs = []
        self.n_ins = 0
        self.uid = 0

    def _ctx(self):
        if self.phase_stack is not None:
            return self.phase_stack
        if getattr(self, "alloc_es", None) is not None:
            return self.alloc_es
        return self.es

    def _mk(self, name, t):
        b = Buf(name, t)
        if self.phase_stack is not None:
            self.phase_bufs.append(b)
        return b

    def sb(self, name, shape, dt):
        self.uid += 1
        name = "%s_%d" % (name, self.uid)
        return self._mk(name, self._ctx().enter_context(self.nc.sbuf_tensor(name, list(shape), dt)))

    def ps(self, name, shape, dt):
        self.uid += 1
        name = "%s_%d" % (name, self.uid)
        b = self._mk(name, self._ctx().enter_context(self.nc.psum_tensor(name, list(shape), dt)))
        b.psum = True
        return b

    @contextlib.contextmanager
    def resident(self):
        saved = self.es
        self.res = contextlib.ExitStack()
        real = self.es

        class _Proxy:
            def enter_context(_s, cm):
                return self.res.enter_context(cm)
        self.alloc_es = _Proxy()
        try:
            yield
        finally:
            self.barrier()
            self.res.close()
            self.alloc_es = None

    @contextlib.contextmanager
    def phase(self):
        assert self.phase_stack is None
        self.phase_stack = contextlib.ExitStack()
        self.phase_bufs = []
        try:
            yield
        finally:
            self.barrier()
            for b in self.phase_bufs:
                if b.ds is not None:
                    self.dpool.append(b.ds)
                    b.ds = None
            self.phase_stack.close()
            self.phase_stack = None
            self.phase_bufs = []

    def _dslot(self, b):
        if b.ds is None:
            if self.dpool:
                b.ds = self.dpool.pop()
            else:
                k = "d%d" % len(self.dall)
                b.ds = [self.es.enter_context(self.nc.semaphore(k)), 0, k]
                self.dall.append(b.ds)
        return b.ds

    def _wait(self, e, tok):
        sem, val, key = tok
        if e == "pe" and key == "pe":
            return
        if self.seen[e].get(key, 0) >= val:
            return
        self.eng[e].wait_ge(sem, val)
        self.seen[e][key] = val

    def op(self, e, fn, reads=(), writes=()):
        for b in reads:
            if b.w is not None:
                self._wait(e, b.w)
            if b.psum:
                for t in b.r.values():
                    self._wait(e, t)
        for b in writes:
            if b.w is not None:
                self._wait(e, b.w)
            for t in b.r.values():
                self._wait(e, t)
        ins = fn(self.eng[e])
        self.cnt[e] += 1
        ins.then_inc(self.sem[e], 1)
        self.n_ins += 1
        tok = (self.sem[e], self.cnt[e], e)
        for b in reads:
            b.r[e] = tok
        for b in writes:
            b.w = tok
            b.r = {}

    def dma(self, q, out, in_, buf, load, **kw):
        ds = self._dslot(buf)
        if load:
            if buf.w is not None and buf.w[2] != ds[2]:
                self._wait(q, buf.w)
            for t in buf.r.values():
                self._wait(q, t)
        else:
            if buf.w is not None:
                self._wait(q, buf.w)
        ins = self.eng[q].dma_start(out=out, in_=in_, **kw)
        ds[1] += 16
        ins.then_inc(ds[0], 16)
        self.n_ins += 1
        tok = (ds[0], ds[1], ds[2])
        if load:
            buf.w = tok
            buf.r = {}
        else:
            buf.r[ds[2]] = tok

    def barrier(self):
        for e in self.eng:
            for e2 in self.eng:
                if e2 != e and self.cnt[e2] > 0:
                    self._wait(e, (self.sem[e2], self.cnt[e2], e2))
            for ds in self.dall:
                if ds[1] > 0:
                    self._wait(e, (ds[0], ds[1], ds[2]))

    def finish(self):
        self.barrier()
        self.es.close()


class Prog:
    def __init__(self):
        self.nc = bass.Bass("TRN2", target_bir_lowering=False)
        self.K = KB(self.nc)
        K = self.K
        self.ins = {}
        self.ident_f = K.sb("identf", [128, 128], F32)
        self.ident_b = K.sb("identb", [128, 128], BF16)
        idn = self.din("ident", [128, 128])
        K.dma("sp", self.ident_f[:], idn, self.ident_f, True)
        K.op("dve", lambda e: e.tensor_copy(out=self.ident_b[:], in_=self.ident_f[:]), [self.ident_f], [self.ident_b])

    def din(self, name, shape, dt=F32):
        return self.nc.dram_tensor(name, list(shape), dt, kind="ExternalInput").ap()

    def dout(self, name, shape, dt=F32):
        return self.nc.dram_tensor(name, list(shape), dt, kind="ExternalOutput").ap()

    def dscr(self, name, shape, dt=F32):
        kind = "ExternalOutput" if name in DEBUG_OUT else "Internal"
        return self.nc.dram_tensor(name, list(shape), dt, kind=kind).ap()

    def mod_phase(self, cpk, ccpk, wmod, bmod, MODL, MODC):
        K = self.K
        with K.phase():
            ones = K.sb("ones1", [1, 128], F32)
            K.op("dve", lambda e: e.memset(ones[:], 1.0), [], [ones])
            bm = K.sb("bm", [1, 6144], F32)
            K.dma("sp", bm[:], bmod, bm, True)
            Ls = []
            for nm, src in (("l", cpk), ("c", ccpk)):
                cp = K.sb("cp" + nm, [128, 8], F32)
                K.dma("sp", cp[:], src, cp, True)
                sg = K.sb("sg" + nm, [128, 8], F32)
                K.op("act", lambda e: e.activation(out=sg[:], in_=cp[:], func=AF.Sigmoid), [cp], [sg])
                cs = K.sb("cs" + nm, [128, 8], F32)
                K.op("dve", lambda e: e.tensor_tensor(out=cs[:], in0=cp[:], in1=sg[:], op=ALU.mult), [cp, sg], [cs])
                L = K.sb("L" + nm, [128, 8, 128], F32)
                K.op("dve", lambda e: e.tensor_copy(out=L[:], in_=cs[:].unsqueeze(2).to_broadcast([128, 8, 128])), [cs], [L])
                Ls.append(L)
            wch = [K.sb("wch%d" % i, [128, 8, 512], F32) for i in range(2)]
            pss = [K.ps("pm%d" % i, [128, 512], F32) for i in range(2)]
            os_ = [K.sb("om%d" % i, [128, 512], F32) for i in range(2)]
            wv = wmod.rearrange("(k p) n -> p k n", p=128)
            for n in range(12):
                w = wch[n % 2]
                K.dma("sp", w[:], wv[:, :, n * 512:(n + 1) * 512], w, True)
                for i, (L, MOD) in enumerate(((Ls[0], MODL), (Ls[1], MODC))):
                    p = pss[i]
                    for k in range(8):
                        K.op("pe", lambda e: e.matmul(out=p[:], lhsT=L[:, k, :], rhs=w[:, k, :], start=(k == 0), stop=False), [L, w], [p])
                    K.op("pe", lambda e: e.matmul(out=p[:], lhsT=ones[0:1, :], rhs=bm[0:1, n * 512:(n + 1) * 512], start=False, stop=True), [ones, bm], [p])
                    o = os_[i]
                    K.op("act" if i == 0 else "dve", (lambda e: e.copy(out=o[:], in_=p[:])) if i == 0 else (lambda e: e.tensor_copy(out=o[:], in_=p[:])), [p], [o])
                    K.dma("sp", MOD[:, n * 512:(n + 1) * 512], o[:], o, False)

    def make_AB(self, MOD, off_shift, off_scale, gain, tag, g=None):
        K = self.K
        A = K.sb("A" + tag, [128, D], F32)
        B = K.sb("B" + tag, [128, D], F32)
        if g is None:
            g = K.sb("g" + tag, [128, D], F32)
        K.dma("sp", A[:], MOD[:, off_scale:off_scale + D], A, True)
        K.dma("sp", B[:], MOD[:, off_shift:off_shift + D], B, True)
        K.dma("sp", g[:], gain.partition_broadcast(128), g, True)
        K.op("dve", lambda e: e.scalar_tensor_tensor(out=A[:], in0=A[:], scalar=1.0, in1=g[:], op0=ALU.add, op1=ALU.mult), [A, g], [A])
        return A, B

    def norm_bufs(self, tag):
        K = self.K
        return dict(sq=K.sb("sq" + tag, [128, D], BF16), ss=K.sb("ss" + tag, [128, 1], F32), rs=K.sb("rs" + tag, [128, 1], F32))

    def rstd(self, xb, nb, n=D):
        K = self.K
        sq, ss, rs = nb["sq"], nb["ss"], nb["rs"]
        K.op("act", lambda e: e.activation(out=sq[:, 0:n], in_=xb, func=AF.Square, accum_out=ss[:]), [], [sq, ss])
        K.op("dve", lambda e: e.tensor_scalar(out=rs[:], in0=ss[:], scalar1=1.0 / n, scalar2=1e-6, op0=ALU.mult, op1=ALU.add), [ss], [rs])
        K.op("act", lambda e: e.activation(out=rs[:], in_=rs[:], func=AF.Sqrt), [rs], [rs])
        K.op("dve", lambda e: e.reciprocal(out=rs[:], in_=rs[:]), [rs], [rs])
        return rs

    def modulate(self, xb, A, B, out, nb):
        K = self.K
        sq, ss, rs, tmp = nb["sq"], nb["ss"], nb["rs"], xb
        K.op("act", lambda e: e.activation(out=sq[:], in_=xb[:], func=AF.Square, accum_out=ss[:]), [xb], [sq, ss])
        K.op("dve", lambda e: e.tensor_scalar(out=rs[:], in0=ss[:], scalar1=1.0 / D, scalar2=1e-6, op0=ALU.mult, op1=ALU.add), [ss], [rs])
        K.op("act", lambda e: e.activation(out=rs[:], in_=rs[:], func=AF.Sqrt), [rs], [rs])
        K.op("dve", lambda e: e.reciprocal(out=rs[:], in_=rs[:]), [rs], [rs])
        K.op("dve", lambda e: e.scalar_tensor_tensor(out=tmp[:], in0=xb[:], scalar=rs[:], in1=A[:], op0=ALU.mult, op1=ALU.mult), [xb, rs, A], [tmp])
        K.op("pool", lambda e: e.tensor_tensor(out=out[:], in0=tmp[:], in1=B[:], op=ALU.add), [tmp, B], [out])

    def transpose8(self, src, tp, dst, eng="act"):
        K = self.K
        for k in range(8):
            K.op("pe", lambda e: e.transpose(out=tp[:, k, :], in_=src[:, k * 128:(k + 1) * 128], identity=self.ident_b[:]), [src, self.ident_b], [tp])
        if eng == "act":
            K.op("act", lambda e: e.copy(out=dst[:], in_=tp[:]), [tp], [dst])
        else:
            K.op("dve", lambda e: e.tensor_copy(out=dst[:], in_=tp[:]), [tp], [dst])

    def load_w_bf16(self, dst, wsrc, ncols, col0=0, tag="w"):
        K = self.K
        wv = wsrc.rearrange("(k p) n -> p k n", p=128)
        st = [K.sb("wst%s%d" % (tag, i), [128, 8, 128], F32) for i in range(2)]
        for i in range(ncols // 128):
            s = st[i % 2]
            K.dma("sp", s[:], wv[:, :, col0 + i * 128:col0 + (i + 1) * 128], s, True)
            K.op("act" if i % 2 == 0 else "dve",
                 (lambda e: e.copy(out=dst[:, :, i * 128:(i + 1) * 128], in_=s[:])) if i % 2 == 0 else
                 (lambda e: e.tensor_copy(out=dst[:, :, i * 128:(i + 1) * 128], in_=s[:])), [s], [dst])

    def peer_convert(self, uT, v, uTb, vb):
        K = self.K
        with K.phase():
            su = [K.sb("su%d" % i, [128, 8, 512], F32) for i in range(2)]
            sub = [K.sb("sub%d" % i, [128, 8, 512], BF16) for i in range(2)]
            sv = [K.sb("sv%d" % i, [128, 4, 1024], F32) for i in range(2)]
            svb = [K.sb("svb%d" % i, [128, 4, 1024], BF16) for i in range(2)]
            uv = uT.rearrange("(k p) e -> p k e", p=128)
            uvb = uTb.rearrange("(k p) e -> p k e", p=128)
            vv = v.rearrange("(i j) d -> j i d", j=128)
            vvb = vb.rearrange("(i j) d -> j i d", j=128)
            for g in range(32):
                a, ab = su[g % 2], sub[g % 2]
                K.dma("sp", a[:], uv[:, :, g * 512:(g + 1) * 512], a, True)
                K.op("act", lambda e: e.copy(out=ab[:], in_=a[:]), [a], [ab])
                K.dma("sp", uvb[:, :, g * 512:(g + 1) * 512], ab[:], ab, False)
                c, cb = sv[g % 2], svb[g % 2]
                K.dma("sp", c[:], vv[:, g * 4:(g + 1) * 4, :], c, True)
                K.op("dve", lambda e: e.tensor_copy(out=cb[:, 0:2, :], in_=c[:, 0:2, :]), [c], [cb])
                K.op("pool", lambda e: e.tensor_copy(out=cb[:, 2:4, :], in_=c[:, 2:4, :]), [c], [cb])
                K.dma("sp", vvb[:, g * 4:(g + 1) * 4, :], cb[:], cb, False)

    def peer_pre(self, srcs, A, B, wq, skT, HT, PA):
        K = self.K
        if True:
            wqs = K.sb("wqs", [128, 8, 2048], F32)
            wv = wq.rearrange("(k p) n -> p k n", p=128)
            for i in range(4):
                K.dma("sp", wqs[:, :, i * 512:(i + 1) * 512], wv[:, :, i * 512:(i + 1) * 512], wqs, True)
            sk = K.sb("sk", [128, 16, 128], F32)
            K.dma("sp", sk[:], skT.rearrange("g k n -> k g n"), sk, True)
            nb = self.norm_bufs("pp")
            xts = [K.sb("ppx%d" % i, [128, D], F32) for i in range(2)]
            hf = K.sb("pph", [128, D], F32)
            hTf = K.sb("pphT", [128, 8, 128], F32)
            hTb = K.sb("pphTb", [128, 8, 128], BF16)
            tpf = K.ps("pptp", [128, 8, 128], F32)
            qps = K.ps("ppq", [128, 16, 128], F32)
            sps = K.ps("pps", [128, 8, 128], F32)
            qT = K.sb("ppqT", [128, 16, 128], F32)
            ssb = K.sb("ppss", [128, 16, 128], F32)
            top = K.sb("pptop", [128, 16, 16], F32)
            tmpa = K.sb("pptmpa", [128, 128], F32)
            cand = K.sb("ppcand", [128, 8, 256], F32)
            tmpc = K.sb("pptmpc", [128, 256], F32)
            best = K.sb("ppbest", [128, 8, 24], F32)
            sm = K.sb("ppsm", [128, 8, 8], F32)
            e16 = K.sb("ppe16", [128, 8, 16], F32)
            pa = K.sb("pppa", [128, 4, D], F32)
            pen = K.sb("pppen", [128, D], F32)
            for t, src in enumerate(srcs):
                if PRE_STOP == 0:
                    continue
                xb = xts[t % 2]
                K.dma("sp", xb[:], src, xb, True)
                self.modulate(xb, A, B, hf, nb)
                if PRE_STOP == 1:
                    K.dma("sp", PA[t][:, 0, :], hf[:], hf, False)
                    continue
                for k in range(8):
                    K.op("pe", lambda e: e.matmul(out=tpf[:, k, :], lhsT=hf[:, k * 128:(k + 1) * 128], rhs=self.ident_f[:], start=True, stop=True), [hf, self.ident_f], [tpf])
                if PRE_STOP == -1:
                    continue
                K.op("act", lambda e: e.copy(out=hTf[:], in_=tpf[:]), [tpf], [hTf])
                if PRE_STOP == -2:
                    continue
                K.op("dve", lambda e: e.tensor_copy(out=hTb[:], in_=tpf[:]), [tpf], [hTb])
                if PRE_STOP == -3:
                    continue
                K.dma("sp", HT[t], hTb[:], hTb, False)
                if PRE_STOP <= 1:
                    continue
                for g in range(16):
                    for k in range(8):
                        K.op("pe", lambda e: e.matmul(out=qps[:, g, :], lhsT=wqs[:, k, g * 128:(g + 1) * 128], rhs=hTf[:, k, :], start=(k == 0), stop=(k == 7)), [wqs, hTf], [qps])
                K.op("act", lambda e: e.copy(out=qT[:, 0:8, :], in_=qps[:, 0:8, :]), [qps], [qT])
                K.op("dve", lambda e: e.tensor_copy(out=qT[:, 8:16, :], in_=qps[:, 8:16, :]), [qps], [qT])
                if PRE_STOP <= 2:
                    continue
                for half in range(2):
                    for gg in range(8):
                        g = half * 8 + gg
                        K.op("pe", lambda e: e.matmul(out=sps[:, gg, :], lhsT=qT[:, g, :], rhs=sk[:, g, :], start=True, stop=True), [qT, sk], [sps])
                    K.op("act", lambda e: e.copy(out=ssb[:, half * 8:(half + 1) * 8, :], in_=sps[:]), [sps], [ssb])
                if PRE_STOP <= 3:
                    continue
                for g in range(16):
                    K.op("dve", lambda e: e.max(out=top[:, g, 0:8], in_=ssb[:, g, :]), [ssb], [top])
                    K.op("dve", lambda e: e.match_replace(out=tmpa[:], in_to_replace=top[:, g, 0:8], in_values=ssb[:, g, :], imm_value=-BIG), [top, ssb], [tmpa])
                    K.op("dve", lambda e: e.max(out=top[:, g, 8:16], in_=tmpa[:]), [tmpa], [top])
                if PRE_STOP <= 4:
                    continue
                tv = top[:].rearrange("p (h two) k -> p h two k", two=2)
                K.op("dve", lambda e: e.tensor_tensor(out=cand[:].rearrange("p h (a b) -> p h a b", b=16),
                                                      in0=tv[:, :, 0, :].unsqueeze(3).to_broadcast([128, 8, 16, 16]),
                                                      in1=tv[:, :, 1, :].unsqueeze(2).to_broadcast([128, 8, 16, 16]), op=ALU.add), [top], [cand])
                for h in range(8):
                    K.op("dve", lambda e: e.max(out=best[:, h, 0:8], in_=cand[:, h, :]), [cand], [best])
                    K.op("dve", lambda e: e.match_replace(out=tmpc[:], in_to_replace=best[:, h, 0:8], in_values=cand[:, h, :], imm_value=-BIG), [best, cand], [tmpc])
                    K.op("dve", lambda e: e.max(out=best[:, h, 8:16], in_=tmpc[:]), [tmpc], [best])
                    K.op("dve", lambda e: e.match_replace(out=tmpc[:], in_to_replace=best[:, h, 8:16], in_values=tmpc[:], imm_value=-BIG), [best, tmpc], [tmpc])
                    K.op("dve", lambda e: e.max(out=best[:, h, 16:24], in_=tmpc[:]), [tmpc], [best])
                if PRE_STOP <= 5:
                    continue
                K.op("dve", lambda e: e.tensor_tensor(out=sm[:, :, 0], in0=best[:, :, 15], in1=best[:, :, 16], op=ALU.add), [best], [sm])
                K.op("dve", lambda e: e.tensor_scalar(out=sm[:, :, 0], in0=sm[:, :, 0], scalar1=0.5, scalar2=None, op0=ALU.mult), [sm], [sm])
                K.op("dve", lambda e: e.tensor_tensor(out=e16[:], in0=best[:, :, 0:16], in1=best[:, :, 0:1].to_broadcast([128, 8, 16]), op=ALU.subtract), [best], [e16])
                K.op("act", lambda e: e.activation(out=e16[:], in_=e16[:], func=AF.Exp), [e16], [e16])
                K.op("dve", lambda e: e.tensor_reduce(out=sm[:, :, 2], in_=e16[:], axis=AX.X, op=ALU.add), [e16], [sm])
                K.op("dve", lambda e: e.reciprocal(out=sm[:, :, 3], in_=sm[:, :, 2]), [sm], [sm])
                if PRE_STOP <= 6:
                    continue
                s0 = ssb[:].rearrange("p (h two) n -> p h two n", two=2)[:, :, 0, :]
                s1 = ssb[:].rearrange("p (h two) n -> p h two n", two=2)[:, :, 1, :]
                pav = pa[:].rearrange("p f (h n) -> p f h n", n=128)
                penv = pen[:].rearrange("p (h n) -> p h n", n=128)

                def bc(ap):
                    return ap.to_broadcast([128, 8, 128])
                K.op("dve", lambda e: e.tensor_tensor(out=pav[:, 0], in0=s0, in1=bc(tv[:, :, 0, 0:1]), op=ALU.subtract), [ssb, top], [pa])
                K.op("act", lambda e: e.activation(out=pa[:, 0, :], in_=pa[:, 0, :], func=AF.Exp), [pa], [pa])
                K.op("dve", lambda e: e.tensor_tensor(out=pav[:, 1], in0=s1, in1=bc(tv[:, :, 1, 0:1]), op=ALU.subtract), [ssb, top], [pa])
                K.op("act", lambda e: e.activation(out=pa[:, 1, :], in_=pa[:, 1, :], func=AF.Exp), [pa], [pa])
                K.op("dve", lambda e: e.tensor_tensor(out=pav[:, 1], in0=pav[:, 1], in1=bc(sm[:, :, 3:4]), op=ALU.mult), [pa, sm], [pa])
                K.op("dve", lambda e: e.tensor_tensor(out=pav[:, 2], in0=bc(sm[:, :, 0:1]), in1=s0, op=ALU.subtract), [ssb, sm], [pa])
                K.op("dve", lambda e: e.tensor_tensor(out=penv, in0=s0, in1=bc(tv[:, :, 0, 15:16]), op=ALU.is_lt), [ssb, top], [pen])
                K.op("dve", lambda e: e.scalar_tensor_tensor(out=pa[:, 2, :], in0=pen[:], scalar=BIG, in1=pa[:, 2, :], op0=ALU.mult, op1=ALU.add), [pen, pa], [pa])
                K.op("dve", lambda e: e.tensor_tensor(out=penv, in0=s1, in1=bc(tv[:, :, 1, 15:16]), op=ALU.is_lt), [ssb, top], [pen])
                K.op("dve", lambda e: e.scalar_tensor_tensor(out=pav[:, 3], in0=penv, scalar=-BIG, in1=s1, op0=ALU.mult, op1=ALU.add), [pen, ssb], [pa])
                K.dma("sp", PA[t], pa[:], pa, False)

    def peer_main(self, srcs, dsts, n_tiles, HT, PA, uTb, vb, GF, final_g=None):
        K = self.K
        with K.phase():
            gf = K.sb("pmgf", [128, D], F32)
            K.dma("sp", gf[:], GF, gf, True)
            fg = None
            if final_g is not None:
                fg = K.sb("pmfg", [128, D], F32)
                K.dma("sp", fg[:], final_g.partition_broadcast(128), fg, True)
                nb = self.norm_bufs("pmn")
            ub = [K.sb("pmu%d" % i, [128, 8, 512], BF16) for i in range(2)]
            vbs = [K.sb("pmv%d" % i, [128, 4, D], BF16) for i in range(2)]
            pas = [K.sb("pmpa%d" % i, [128, 4, D], F32) for i in range(2)]
            hts = [K.sb("pmht%d" % i, [128, 8, 128], BF16) for i in range(2)]
            outp = [K.ps("pmo%d" % i, [128, 2, 512], F32) for i in range(2)]
            aps = [K.ps("pma%d" % i, [128, 512], F32) for i in range(2)]
            tps = K.ps("pmt", [128, 8, 128], BF16)
            ga = [K.sb("pmga%d" % i, [128, 512], F32) for i in range(2)]
            MB = [K.sb("pmMB%d" % i, [128, 128], F32) for i in range(4)]
            acc = [K.sb("pmacc%d" % i, [128, 128], F32) for i in range(4)]
            Wb = [K.sb("pmW%d" % i, [128, 128], BF16) for i in range(4)]
            WT = [K.sb("pmWT%d" % i, [128, 128], BF16) for i in range(4)]
            xr = K.sb("pmx", [128, D], F32)
            xo = K.sb("pmxo", [128, D], F32)
            uvb = uTb.rearrange("(k p) e -> p k e", p=128)
            vvb = vb.rearrange("(i j) d -> j i d", j=128)
            for blk in range(n_tiles // 2):
                for tt in range(2):
                    t = blk * 2 + tt
                    K.dma("sp", pas[tt][:], PA[t], pas[tt], True)
                    K.dma("sp", hts[tt][:], HT[t], hts[tt], True)
                for ig in range(32):
                    u_, v_ = ub[ig % 2], vbs[ig % 2]
                    K.dma("sp", u_[:], uvb[:, :, ig * 512:(ig + 1) * 512], u_, True)
                    K.dma("sp", v_[:], vvb[:, ig * 4:(ig + 1) * 4, :], v_, True)
                    for tt in range(2):
                        pa, ht = pas[tt], hts[tt]
                        ap_, ga_ = aps[tt], ga[tt]
                        for k in range(8):
                            K.op("pe", lambda e: e.matmul(out=ap_[:], lhsT=ht[:, k, :], rhs=u_[:, k, :], start=(k == 0), stop=(k == 7)), [ht, u_], [ap_])
                        K.op("act", lambda e: e.activation(out=ga_[:], in_=ap_[:], func=GELU), [ap_], [ga_])
                        for h in range(8):
                            for ii in range(4):
                                i = ig * 4 + ii
                                K.op("dve", lambda e: e.scalar_tensor_tensor(out=MB[ii][:], in0=pa[:, 3, h * 128:(h + 1) * 128], scalar=pa[:, 2, h * 128 + i:h * 128 + i + 1],
                                                                             in1=pa[:, 1, h * 128:(h + 1) * 128], op0=ALU.is_ge, op1=ALU.mult), [pa], [MB[ii]])
                            for ii in range(4):
                                i = ig * 4 + ii
                                if h == 0:
                                    K.op("dve", lambda e: e.tensor_scalar(out=acc[ii][:], in0=MB[ii][:], scalar1=pa[:, 0, i:i + 1], scalar2=None, op0=ALU.mult), [MB[ii], pa], [acc[ii]])
                                else:
                                    K.op("dve", lambda e: e.scalar_tensor_tensor(out=acc[ii][:], in0=MB[ii][:], scalar=pa[:, 0, h * 128 + i:h * 128 + i + 1], in1=acc[ii][:],
                                                                                 op0=ALU.mult, op1=ALU.add), [MB[ii], pa, acc[ii]], [acc[ii]])
                        for ii in range(4):
                            i = ig * 4 + ii
                            K.op("pool", lambda e: e.tensor_tensor(out=Wb[ii][:], in0=ga_[:, ii * 128:(ii + 1) * 128], in1=acc[ii][:], op=ALU.mult), [ga_, acc[ii]], [Wb[ii]])
                            K.op("pe", lambda e: e.transpose(out=tps[:, ii, :], in_=Wb[ii][:], identity=self.ident_b[:]), [Wb[ii], self.ident_b], [tps])
                            K.op("act", lambda e: e.copy(out=WT[ii][:], in_=tps[:, ii, :]), [tps], [WT[ii]])
                            for hf_ in range(2):
                                K.op("pe", lambda e: e.matmul(out=outp[tt][:, hf_, :], lhsT=WT[ii][:], rhs=v_[:, ii, hf_ * 512:(hf_ + 1) * 512], start=(i == 0), stop=(i == 127)),
                                     [WT[ii], v_], [outp[tt]])
                for tt in range(2):
                    t = blk * 2 + tt
                    K.dma("sp", xr[:], srcs[t], xr, True)
                    K.op("dve", lambda e: e.tensor_tensor(out=xo[:], in0=outp[tt][:].rearrange("p a b -> p (a b)"), in1=gf[:], op=ALU.mult), [outp[tt], gf], [xo])
                    K.op("pool", lambda e: e.tensor_tensor(out=xo[:], in0=xo[:], in1=xr[:], op=ALU.add), [xo, xr], [xo])
                    if fg is not None:
                        self.modulate_final(xo, fg, nb)
                    K.dma("sp", dsts[t], xo[:], xo, False)

    def modulate_final(self, xo, fg, nb):
        K = self.K
        sq, ss, rs = nb["sq"], nb["ss"], nb["rs"]
        K.op("act", lambda e: e.activation(out=sq[:], in_=xo[:], func=AF.Square, accum_out=ss[:]), [xo], [sq, ss])
        K.op("dve", lambda e: e.tensor_scalar(out=rs[:], in0=ss[:], scalar1=1.0 / D, scalar2=1e-6, op0=ALU.mult, op1=ALU.add), [ss], [rs])
        K.op("act", lambda e: e.activation(out=rs[:], in_=rs[:], func=AF.Sqrt), [rs], [rs])
        K.op("dve", lambda e: e.reciprocal(out=rs[:], in_=rs[:]), [rs], [rs])
        K.op("dve", lambda e: e.scalar_tensor_tensor(out=xo[:], in0=xo[:], scalar=rs[:], in1=fg[:], op0=ALU.mult, op1=ALU.mult), [xo, rs, fg], [xo])

    def peer_layer(self, srcs, dsts, MOD, gain, wq, skT, uT, v, tag, final_g=None):
        K = self.K
        n = len(srcs)
        assert n % 2 == 0
        HT = self.dscr("HT" + tag, [n, 128, 8, 128], BF16)
        PA = self.dscr("PA" + tag, [n, 128, 4, D])
        with K.phase():
            A, B = self.make_AB(MOD, 3 * D, 4 * D, gain, "pf" + tag)
            self.peer_pre(srcs, A, B, wq, skT, HT, PA)
        self.peer_main(srcs, dsts, n, HT, PA, uT, v, MOD[:, 5 * D:6 * D], final_g=final_g)


def build_A(stop_after=None, lat_tiles=32):
    P = Prog()
    K = P.K
    xe = P.din("xe", [NEXT * 128, D])
    xf = P.din("xf", [SEQ, D])
    ctx = P.din("ctx", [NCTX, D])
    cpk = P.din("cpk", [128, 8])
    ccpk = P.din("ccpk", [128, 8])
    wmod = P.din("wmod", [D, 6 * D])
    bmod = P.din("bmod", [1, 6 * D])
    gmix = P.din("gmix", [D])
    gffn = P.din("gffn", [D])
    win = P.din("win", [D, 2048])
    wout = P.din("wout", [D, D])
    nab = P.din("nab", [5, 128, 8, 7, 128])
    ccs = P.din("ccs", [128, 256])
    c1 = P.din("c1", [128, 128])
    s1 = P.din("s1", [128, 128])
    ns1 = P.din("ns1", [128, 128])
    tw = P.din("tw", [128, 128])
    c2m = P.din("c2m", [64, 32])
    ns2m = P.din("ns2m", [64, 32])
    c256 = P.din("c256", [256, 256])
    ns256 = P.din("ns256", [256, 256])
    wq = P.din("wq", [D, 2048])
    skT = P.din("skT", [16, 128, 128])
    uT = P.din("uT", [D, 16384])
    pv = P.din("pv", [16384, D])
    xo = P.dout("xo", [HALF, D])
    xco = P.dout("xco", [NCTX, D])

    MODL = P.dscr("MODL", [128, 6 * D])
    MODC = P.dscr("MODC", [128, 6 * D])
    G = P.dscr("G", [SEQ, D])
    Gc = P.dscr("Gc", [NCTX, D])
    Bsc = P.dscr("Bsc", [128, 64, D])
    Y = P.dscr("Y", [HALF, 512])
    X1 = P.dscr("X1", [HALF, D])
    XC1 = P.dscr("XC1", [NCTX, D])
    uTb = P.dscr("uTb", [D, 16384], BF16)
    vb = P.dscr("vb", [16384, D], BF16)

    P.mod_phase(cpk, ccpk, wmod, bmod, MODL, MODC)
    if stop_after == "mod":
        return P, dict(MODL=MODL, MODC=MODC)

    NT1 = NEXT + 2
    res_cm = K.resident()
    res_cm.__enter__()
    QT = K.sb("QT", [128, 4, 34 * 128], BF16)

    def qi(t):
        return t - 3 if t < NEXT else 32 + (t - NEXT)
    KT = K.sb("KT", [128, 4, NT1 * 128], BF16)
    Vp = K.sb("Vp", [128, NT1, 8 * 65], BF16)
    K.op("pool", lambda e: e.memset(Vp[:], 1.0), [], [Vp])

    with K.phase():
        gtmp = K.sb("gtmp", [128, D], F32)
        Al, Bl = P.make_AB(MODL, 0, D, gmix, "l", g=gtmp)
        Ac, Bc = P.make_AB(MODC, 0, D, gmix, "c", g=gtmp)
        wb = K.sb("winb", [128, 8, 2048], BF16)
        P.load_w_bf16(wb, win, 2048)
        ccs_s = K.sb("ccs", [128, 256], F32)
        K.dma("sp", ccs_s[:], ccs, ccs_s, True)
        nb = P.norm_bufs("a")
        xts = [K.sb("xt%d" % i, [128, D], F32) for i in range(2)]
        hb = K.sb("hb", [128, D], BF16)
        hT = K.sb("hT", [128, 8, 128], BF16)
        tp = K.ps("tp", [128, 8, 128], BF16)
        psq = K.ps("psq", [128, 4, 128], F32)
        psk = K.ps("psk", [128, 4, 128], F32)
        psv = K.ps("psv", [128, 512], F32)
        psf = K.ps("psf", [128, 4, 128], F32)
        psg = K.ps("psg", [128, 4, 256], F32)
        FT = K.sb("FT", [128, 4, 128], F32)
        Gt = K.sb("Gt", [128, D], F32)

        def fourier_cols(t_dst):
            for g in range(4):
                for k in range(8):
                    K.op("pe", lambda e: e.matmul(out=psf[:, g, :], lhsT=wb[:, k, 1536 + g * 128:1536 + (g + 1) * 128], rhs=hT[:, k, :], start=(k == 0), stop=(k == 7)), [wb, hT], [psf])
            K.op("act", lambda e: e.copy(out=FT[:], in_=psf[:]), [psf], [FT])
            for g in range(4):
                K.op("pe", lambda e: e.matmul(out=psg[:, g, :], lhsT=FT[:, g, :], rhs=ccs_s[:], start=True, stop=True), [FT, ccs_s], [psg])
            K.op("dve", lambda e: e.tensor_copy(out=Gt[:], in_=psg[:].rearrange("p g c -> p (g c)")), [psg], [Gt])
            K.dma("sp", t_dst, Gt[:], Gt, False)

        for t in range(NT1):
            xb = xts[t % 2]
            isctx = t >= NEXT
            src = ctx[(t - NEXT) * 128:(t - NEXT + 1) * 128, :] if isctx else xe[t * 128:(t + 1) * 128, :]
            K.dma("sp", xb[:], src, xb, True)
            P.modulate(xb, Ac if isctx else Al, Bc if isctx else Bl, hb, nb)
            P.transpose8(hb, tp, hT)
            if isctx or 3 <= t < 35:
                for fp in range(4):
                    for k in range(8):
                        K.op("pe", lambda e: e.matmul(out=psq[:, fp, :], lhsT=wb[:, k, fp * 128:(fp + 1) * 128], rhs=hT[:, k, :], start=(k == 0), stop=(k == 7)), [wb, hT], [psq])
                K.op("act", lambda e: e.copy(out=QT[:, :, qi(t) * 128:(qi(t) + 1) * 128], in_=psq[:]), [psq], [QT])
            for fp in range(4):
                for k in range(8):
                    K.op("pe", lambda e: e.matmul(out=psk[:, fp, :], lhsT=wb[:, k, 512 + fp * 128:512 + (fp + 1) * 128], rhs=hT[:, k, :], start=(k == 0), stop=(k == 7)), [wb, hT], [psk])
            K.op("act", lambda e: e.copy(out=KT[:, :, t * 128:(t + 1) * 128], in_=psk[:]), [psk], [KT])
            for k in range(8):
                K.op("pe", lambda e: e.matmul(out=psv[:], lhsT=hT[:, k, :], rhs=wb[:, k, 1024:1536], start=(k == 0), stop=(k == 7)), [hT, wb], [psv])
            K.op("dve", lambda e: e.tensor_copy(out=Vp[:, t, :].rearrange("p (h c) -> p h c", c=65)[:, :, 0:64], in_=psv[:].rearrange("p (h c) -> p h c", c=64)), [psv], [Vp])
            if isctx:
                fourier_cols(Gc[(t - NEXT) * 128:(t - NEXT + 1) * 128, :])
        for t in range(SEQ // 128):
            xb = xts[t % 2]
            K.dma("sp", xb[:], xf[t * 128:(t + 1) * 128, :], xb, True)
            P.modulate(xb, Al, Bl, hb, nb)
            P.transpose8(hb, tp, hT)
            fourier_cols(G[t * 128:(t + 1) * 128, :])
    if stop_after == "proj":
        dq = P.dout("dbg_q", [128, 4, 34 * 128], BF16)
        dk = P.dout("dbg_k", [128, 4, NT1 * 128], BF16)
        dv = P.dout("dbg_v", [128, NT1, 8 * 65], BF16)
        K.dma("sp", dq, QT[:], QT, False)
        K.dma("sp", dk, KT[:], KT, False)
        K.dma("sp", dv, Vp[:], Vp, False)
        return P, dict(G=G, Gc=Gc)

    with K.phase():
        c1s = K.sb("c1s", [128, 128], F32)
        s1s = K.sb("s1s", [128, 128], F32)
        ns1s = K.sb("ns1s", [128, 128], F32)
        tws = K.sb("tws", [128, 128], F32)
        for b_, s_ in ((c1s, c1), (s1s, s1), (ns1s, ns1), (tws, tw)):
            K.dma("sp", b_[:], s_, b_, True)
        gin = [K.sb("gin%d" % i, [128, 4, D], F32) for i in range(2)]
        par = K.ps("par", [128, 512], F32)
        pai = K.ps("pai", [128, 512], F32)
        t1 = K.sb("t1", [128, 512], F32)
        t2 = K.sb("t2", [128, 512], F32)
        Bts = [K.sb("Bt%d" % i, [128, D], F32) for i in range(2)]
        Gv = G.rearrange("(a b) f -> a b f", b=64)
        for nn in range(16):
            gi = gin[nn % 2]
            K.dma("sp", gi[:], Gv[:, nn * 4:(nn + 1) * 4, :], gi, True)
            for j in range(4):
                n2 = nn * 4 + j
                gv = gi[:, j, :].rearrange("p (g r m) -> p g r m", g=4, r=2)
                gr, gim = gv[:, :, 0, :], gv[:, :, 1, :]
                K.op("pe", lambda e: e.matmul(out=par[:], lhsT=c1s[:], rhs=gr, start=True, stop=False), [c1s, gi], [par])
                K.op("pe", lambda e: e.matmul(out=par[:], lhsT=ns1s[:], rhs=gim, start=False, stop=True), [ns1s, gi], [par])
                K.op("pe", lambda e: e.matmul(out=pai[:], lhsT=c1s[:], rhs=gim, start=True, stop=False), [c1s, gi], [pai])
                K.op("pe", lambda e: e.matmul(out=pai[:], lhsT=s1s[:], rhs=gr, start=False, stop=True), [s1s, gi], [pai])
                Bt = Bts[n2 % 2]
                tc_, ts_ = tws[:, n2:n2 + 1], tws[:, 64 + n2:64 + n2 + 1]
                K.op("dve", lambda e: e.tensor_scalar(out=t1[:], in0=pai[:], scalar1=ts_, scalar2=None, op0=ALU.mult), [pai, tws], [t1])
                K.op("dve", lambda e: e.scalar_tensor_tensor(out=Bt[:, 0:512], in0=par[:], scalar=tc_, in1=t1[:], op0=ALU.mult, op1=ALU.subtract), [par, tws, t1], [Bt])
                K.op("dve", lambda e: e.tensor_scalar(out=t2[:], in0=par[:], scalar1=ts_, scalar2=None, op0=ALU.mult), [par, tws], [t2])
                K.op("dve", lambda e: e.scalar_tensor_tensor(out=Bt[:, 512:1024], in0=pai[:], scalar=tc_, in1=t2[:], op0=ALU.mult, op1=ALU.add), [pai, tws, t2], [Bt])
                K.dma("sp", Bsc[:, n2, :], Bt[:], Bt, False)

    with K.phase():
        c2s = K.sb("c2s", [64, 32], F32)
        ns2s = K.sb("ns2s", [64, 32], F32)
        K.dma("sp", c2s[:], c2m, c2s, True)
        K.dma("sp", ns2s[:], ns2m, ns2s, True)
        bin_ = [K.sb("bin%d" % i, [64, 4, D], F32) for i in range(2)]
        pys = [K.ps("py%d" % i, [32, 512], F32) for i in range(2)]
        ysb = [K.sb("ysb%d" % i, [32, 4, 512], F32) for i in range(2)]
        Bv = Bsc.rearrange("k n f -> n k f")
        Yv = Y.rearrange("(k2 k1) f -> k2 k1 f", k1=128)
        for kg in range(32):
            bi = bin_[kg % 2]
            K.dma("sp", bi[:], Bv[:, kg * 4:(kg + 1) * 4, :], bi, True)
            ys = ysb[kg % 2]
            for kl in range(4):
                py = pys[kl % 2]
                K.op("pe", lambda e: e.matmul(out=py[:], lhsT=c2s[:], rhs=bi[:, kl, 0:512], start=True, stop=False), [c2s, bi], [py])
                K.op("pe", lambda e: e.matmul(out=py[:], lhsT=ns2s[:], rhs=bi[:, kl, 512:1024], start=False, stop=True), [ns2s, bi], [py])
                K.op("act", lambda e: e.mul(out=ys[:, kl, :], in_=py[:], mul=1.0 / 1024.0), [py], [ys])
            K.dma("sp", Yv[:, kg * 4:(kg + 1) * 4, :], ys[:], ys, False)
    if stop_after == "fft":
        return P, dict(Y=Y)

    with K.phase():
        biasb = K.sb("biasb", [128, 8, 7, 128], BF16)
        bst = [K.sb("bst%d" % i, [128, 1, 7, 128], F32) for i in range(2)]
        woutb = K.sb("woutb", [128, 8, D], BF16)
        P.load_w_bf16(woutb, wout, D, tag="o")
        gm = K.sb("gm", [128, D], F32)
        K.dma("sp", gm[:], MODL[:, 2 * D:3 * D], gm, True)
        cgm = K.sb("cgm", [128, D], F32)
        K.dma("sp", cgm[:], MODC[:, 2 * D:3 * D], cgm, True)
        gcs = K.sb("gcs", [128, 2, D], F32)
        K.dma("sp", gcs[:], Gc.rearrange("(a p) f -> p a f", p=128), gcs, True)
        c256s = K.sb("c256s", [128, 2, 256], F32)
        ns256s = K.sb("ns256s", [128, 2, 256], F32)
        K.dma("sp", c256s[:], c256.rearrange("(a p) f -> p a f", p=128), c256s, True)
        K.dma("sp", ns256s[:], ns256.rearrange("(a p) f -> p a f", p=128), ns256s, True)
        pss = K.ps("pss", [128, 12, 128], F32)
        pso = K.ps("pso", [128, 8, 128], F32)
        tp = K.ps("tp4", [128, 8, 128], BF16)
        psy = K.ps("psy", [128, 2, 512], F32)
        Sb = [K.sb("Sb%d" % i, [128, 7, 128], F32) for i in range(2)]
        PT = [K.sb("PT%d" % i, [128, 9, 128], BF16) for i in range(2)]
        rz = K.sb("rz", [128, 8], F32)
        mix = K.sb("mix", [128, D], BF16)
        mixT = K.sb("mixT", [128, 8, 128], BF16)
        yin = K.sb("yin", [128, 512], F32)
        xr = K.sb("xr4", [128, D], F32)
        x1 = K.sb("x14", [128, D], F32)

        def load_pattern(slot):
            for hh in range(8):
                s = bst[hh % 2]
                K.dma("sp", s[:], nab[slot, :, hh:hh + 1, :, :], s, True)
                K.op("pool", lambda e: e.tensor_copy(out=biasb[:, hh:hh + 1, :, :], in_=s[:]), [s], [biasb])

        def attn_tile(qt, ktiles, nbias, idx):
            nk = len(ktiles)
            for h in range(8):
                hp, pb = h // 2, (h % 2) * 64
                sb_, pt_ = Sb[h % 2], PT[h % 2]
                for j, kt in enumerate(ktiles):
                    K.op("pe", lambda e: e.matmul(out=pss[:, j, :], lhsT=KT[pb:pb + 64, hp, kt * 128:(kt + 1) * 128], rhs=QT[pb:pb + 64, hp, qi(qt) * 128:(qi(qt) + 1) * 128],
                                                  start=True, stop=True), [KT, QT], [pss])
                if nbias:
                    K.op("dve", lambda e: e.scalar_tensor_tensor(out=sb_[:], in0=pss[:, 0:nbias, :], scalar=0.125, in1=biasb[:, h, :, :], op0=ALU.mult, op1=ALU.add), [pss, biasb], [sb_])
                    K.op("act", lambda e: e.activation(out=pt_[:, 0:nbias, :], in_=sb_[:], func=AF.Exp), [sb_], [pt_])
                K.op("act", lambda e: e.activation(out=pt_[:, nbias:nk, :], in_=pss[:, nbias:nk, :], func=AF.Exp, scale=0.125), [pss], [pt_])
                for j, kt in enumerate(ktiles):
                    K.op("pe", lambda e: e.matmul(out=pso[:, h, 0:65], lhsT=pt_[:, j, :], rhs=Vp[:, kt, h * 65:(h + 1) * 65], start=(j == 0), stop=(j == nk - 1)), [pt_, Vp], [pso])
            K.op("dve", lambda e: e.reciprocal(out=rz[:], in_=pso[:, :, 64]), [pso], [rz])
            K.op("dve", lambda e: e.tensor_tensor(out=mix[:, 0:512].rearrange("p (h c) -> p h c", c=64), in0=pso[:, :, 0:64], in1=rz[:].unsqueeze(2).to_broadcast([128, 8, 64]), op=ALU.mult), [pso, rz], [mix])

        def outproj_residual(xsrc, gate, dst):
            P.transpose8(mix, tp, mixT)
            for hf_ in range(2):
                for k in range(8):
                    K.op("pe", lambda e: e.matmul(out=psy[:, hf_, :], lhsT=mixT[:, k, :], rhs=woutb[:, k, hf_ * 512:(hf_ + 1) * 512], start=(k == 0), stop=(k == 7)), [mixT, woutb], [psy])
            K.dma("sp", xr[:], xsrc, xr, True)
            K.op("dve", lambda e: e.tensor_tensor(out=x1[:], in0=psy[:].rearrange("p a b -> p (a b)"), in1=gate[:], op=ALU.mult), [psy, gate], [x1])
            K.op("pool", lambda e: e.tensor_tensor(out=x1[:], in0=x1[:], in1=xr[:], op=ALU.add), [x1, xr], [x1])
            K.dma("sp", dst, x1[:], x1, False)

        for lp in range(32):
            slot = {0: 0, 1: 1, 2: 2, 30: 3, 31: 4}.get(lp)
            if slot is not None:
                load_pattern(slot)
            attn_tile(lp + 3, [lp + j for j in range(7)] + [NEXT, NEXT + 1], 7, lp)
            K.dma("sp", yin[:], Y[lp * 128:(lp + 1) * 128, :], yin, True)
            K.op("act", lambda e: e.copy(out=mix[:, 512:1024], in_=yin[:]), [yin], [mix])
            outproj_residual(xe[(lp + 3) * 128:(lp + 4) * 128, :], gm, X1[lp * 128:(lp + 1) * 128, :])
        for ct in range(2):
            attn_tile(NEXT + ct, [NEXT, NEXT + 1], 0, 100 + ct)
            pyc = psy
            first = True
            for a in range(2):
                gv = gcs[:, a, :].rearrange("p (g r m) -> p g r m", g=4, r=2)
                K.op("pe", lambda e: e.matmul(out=pyc[:, 0, :], lhsT=c256s[:, a, ct * 128:(ct + 1) * 128], rhs=gv[:, :, 0, :], start=first, stop=False), [c256s, gcs], [pyc])
                first = False
                K.op("pe", lambda e: e.matmul(out=pyc[:, 0, :], lhsT=ns256s[:, a, ct * 128:(ct + 1) * 128], rhs=gv[:, :, 1, :], start=False, stop=(a == 1)), [ns256s, gcs], [pyc])
            K.op("act", lambda e: e.mul(out=mix[:, 512:1024], in_=pyc[:, 0, :], mul=1.0 / math.sqrt(256.0 * 128.0)), [pyc], [mix])
            outproj_residual(ctx[ct * 128:(ct + 1) * 128, :], cgm, XC1[ct * 128:(ct + 1) * 128, :])
    res_cm.__exit__(None, None, None)
    if stop_after == "mix":
        return P, dict(X1=X1, XC1=XC1)

    P.peer_convert(uT, pv, uTb, vb)
    srcs = [X1[t * 128:(t + 1) * 128, :] for t in range(lat_tiles)]
    dsts = [xo[t * 128:(t + 1) * 128, :] for t in range(lat_tiles)]
    if lat_tiles:
        P.peer_layer(srcs, dsts, MODL, gffn, wq, skT, uTb, vb, "l")
    srcs = [XC1[t * 128:(t + 1) * 128, :] for t in range(2)]
    dsts = [xco[t * 128:(t + 1) * 128, :] for t in range(2)]
    P.peer_layer(srcs, dsts, MODC, gffn, wq, skT, uTb, vb, "c")
    return P, {}


def _na_pattern(rpb, r):
    H = rpb.shape[0]
    out = np.full((128, H, 7, 128), NEG, np.float32)
    qc = np.arange(64)
    cs = np.clip(qc - 8, 0, 64 - 16)
    for rr in range(2):
        qrow = r + rr
        rs = min(max(qrow - 4, 0), 128 - 8)
        for j in range(7):
            for kr in range(2):
                krow = r - 6 + 2 * j + kr
                if krow < rs or krow >= rs + 8:
                    continue
                drow = krow - qrow + 7
                for kc in range(64):
                    ok = (kc >= cs) & (kc < cs + 16)
                    qs = qc[ok]
                    out[kr * 64 + kc, :, j, rr * 64 + qs] = rpb[:, drow, kc - qs + 15].T
    return out


def _tables(hb):
    t = {}
    t["ident"] = np.eye(128, dtype=np.float32)
    c = np.arange(128, dtype=np.float64)
    ang = 2 * np.pi * np.outer(c, c) / 128.0
    t["ccs"] = np.concatenate([np.cos(ang), np.sin(ang)], axis=1).astype(np.float32)
    t["c1"] = np.cos(ang).astype(np.float32)
    t["s1"] = np.sin(ang).astype(np.float32)
    t["ns1"] = (-np.sin(ang)).astype(np.float32)
    a2 = 2 * np.pi * np.outer(c, np.arange(64, dtype=np.float64)) / 8192.0
    t["tw"] = np.concatenate([np.cos(a2), np.sin(a2)], axis=1).astype(np.float32)
    a3 = 2 * np.pi * np.outer(np.arange(64, dtype=np.float64), np.arange(32, dtype=np.float64) + 32 * hb) / 64.0
    t["c2m"] = np.cos(a3).astype(np.float32)
    t["ns2m"] = (-np.sin(a3)).astype(np.float32)
    n = np.arange(256, dtype=np.float64)
    a4 = 2 * np.pi * np.outer(n, n) / 256.0
    t["c256"] = np.cos(a4).astype(np.float32)
    t["ns256"] = (-np.sin(a4)).astype(np.float32)
    return t


_PAT_CACHE = {}


def prep_A(inp, cid):
    b, hb = cid // 2, cid % 2
    x = inp["x"][b]
    r0 = hb * 64 - 6
    xe = np.zeros((NEXT * 128, D), np.float32)
    lo, hi = max(r0, 0), min(r0 + 76, 128)
    xe[(lo - r0) * 64:(hi - r0) * 64] = x[lo * 64:hi * 64]
    m = dict(_tables(hb))
    m["xe"] = xe
    m["xf"] = np.ascontiguousarray(x)
    m["ctx"] = np.ascontiguousarray(inp["ctx"][b])
    m["cpk"] = np.ascontiguousarray(inp["c"][b].reshape(8, 128).T)
    m["ccpk"] = np.ascontiguousarray(inp["c_ctx"].reshape(8, 128).T)
    m["wmod"] = np.ascontiguousarray(inp["w_mod"][0])
    m["bmod"] = np.ascontiguousarray(inp["b_mod"][0][None, :])
    m["gmix"] = np.ascontiguousarray(inp["norm_mix_g"][0])
    m["gffn"] = np.ascontiguousarray(inp["norm_ffn_g"][0])
    m["win"] = np.ascontiguousarray(inp["even_w_in"][0])
    m["wout"] = np.ascontiguousarray(inp["even_w_out"][0])
    rpb = inp["na_rpb"][0]
    key = rpb.tobytes()[:64]
    if key not in _PAT_CACHE:
        _PAT_CACHE.clear()
        _PAT_CACHE[key] = {r: _na_pattern(rpb, r) for r in (0, 2, 64, 124, 126)}
    pc = _PAT_CACHE[key]
    pats = [pc[0], pc[2], pc[64], pc[64], pc[64]] if hb == 0 else [pc[64], pc[64], pc[64], pc[124], pc[126]]
    m["nab"] = np.stack(pats)
    m["wq"] = np.ascontiguousarray(inp["peer_w_q"][0])
    m["skT"] = np.ascontiguousarray(inp["peer_sub_keys"][0].reshape(16, 128, 128).transpose(0, 2, 1))
    m["uT"] = np.ascontiguousarray(inp["peer_u"][0].T)
    m["pv"] = np.ascontiguousarray(inp["peer_v"][0])
    return m


LAMBDA_INIT = 0.8 - 0.6 * math.exp(-0.3 * 1)
NKC = (SEQ + NCTX) // 128


def build_B(stop_after=None, lat_tiles=32):
    P = Prog()
    K = P.K
    x2f = P.din("x2f", [SEQ, D])
    x2m = P.din("x2m", [HALF, D])
    xc2 = P.din("xc2", [NCTX, D])
    cpk = P.din("cpk", [128, 8])
    ccpk = P.din("ccpk", [128, 8])
    wmod = P.din("wmod", [D, 6 * D])
    bmod = P.din("bmod", [1, 6 * D])
    gmix = P.din("gmix", [D])
    gffn = P.din("gffn", [D])
    win = P.din("win", [D, 3072])
    winp = P.din("winp", [D, 2048])
    wout = P.din("wout", [D, D])
    cosk = P.din("cosk", [128, SEQ])
    sink = P.din("sink", [128, SEQ])
    cosq = P.din("cosq", [128, HALF])
    sinq = P.din("sinq", [128, HALF])
    lam4 = P.din("lam4", [4, 64])
    subg = P.din("subg", [128])
    wq = P.din("wq", [D, 2048])
    skT = P.din("skT", [16, 128, 128])
    uT = P.din("uT", [D, 16384])
    pv = P.din("pv", [16384, D])
    fng = P.din("fng", [D])
    out = P.dout("out", [HALF, D])

    MODL = P.dscr("MODL", [128, 6 * D])
    MODC = P.dscr("MODC", [128, 6 * D])
    KT1 = P.dscr("KT1", [8, 128, SEQ + NCTX], BF16)
    QT1 = P.dscr("QT1", [8, 128, HALF], BF16)
    V1 = P.dscr("V1", [NKC, 128, 8 * 129], BF16)
    AO = P.dscr("AO", [HALF, D], BF16)
    X3 = P.dscr("X3", [HALF, D])
    uTb = P.dscr("uTb", [D, 16384], BF16)
    vb = P.dscr("vb", [16384, D], BF16)

    P.mod_phase(cpk, ccpk, wmod, bmod, MODL, MODC)

    with K.phase():
        gtmp = K.sb("gtmp", [128, D], F32)
        Al, Bl = P.make_AB(MODL, 0, D, gmix, "l", g=gtmp)
        Ac, Bc = P.make_AB(MODC, 0, D, gmix, "c", g=gtmp)
        wb = K.sb("winb", [128, 8, 3072], BF16)
        P.load_w_bf16(wb, win, 3072)
        wpb = K.sb("winpb", [128, 8, 2048], BF16)
        P.load_w_bf16(wpb, winp, 2048, tag="p")
        nb = P.norm_bufs("b")
        xts = [K.sb("xt%d" % i, [128, D], F32) for i in range(2)]
        hb = K.sb("hb", [128, D], BF16)
        hT4 = K.sb("hT4", [128, 8, 512], BF16)
        tp = K.ps("tp", [128, 8, 128], BF16)
        ps1 = [K.ps("ps1%d" % i, [128, 512], F32) for i in range(2)]
        ps2 = [K.ps("ps2%d" % i, [128, 512], F32) for i in range(2)]
        psv = K.ps("psv", [128, 2, 512], F32)
        cs_ = K.sb("cosb", [128, 512], F32)
        sn_ = K.sb("sinb", [128, 512], F32)
        t1 = [K.sb("rt1%d" % i, [128, 512], F32) for i in range(2)]
        t2 = [K.sb("rt2%d" % i, [128, 512], F32) for i in range(2)]
        kb_ = [K.sb("rkb%d" % i, [128, 512], BF16) for i in range(2)]
        vt = [K.sb("vt%d" % i, [128, 8, 129], BF16) for i in range(2)]
        for v_ in vt:
            K.op("pool", lambda e: e.memset(v_[:], 1.0), [], [v_])

        def block(src_tiles, ntok, ctab, stab, col0, dstT, tok0, vchunk0):
            for i, src in enumerate(src_tiles):
                xb = xts[i % 2]
                K.dma("sp", xb[:], src, xb, True)
                P.modulate(xb, block.A, block.B, hb, nb)
                for k in range(8):
                    K.op("pe", lambda e: e.transpose(out=tp[:, k, :], in_=hb[:, k * 128:(k + 1) * 128], identity=P.ident_b[:]), [hb, P.ident_b], [tp])
                K.op("act", lambda e: e.copy(out=hT4[:, :, i * 128:(i + 1) * 128], in_=tp[:]), [tp], [hT4])
                if vchunk0 is not None:
                    for hf_ in range(2):
                        for k in range(8):
                            K.op("pe", lambda e: e.matmul(out=psv[:, hf_, :], lhsT=hT4[:, k, i * 128:(i + 1) * 128], rhs=wb[:, k, 2048 + hf_ * 512:2048 + (hf_ + 1) * 512],
                                                          start=(k == 0), stop=(k == 7)), [hT4, wb], [psv])
                    v_ = vt[i % 2]
                    K.op("dve", lambda e: e.tensor_copy(out=v_[:, :, 0:128], in_=psv[:].rearrange("p a (h c) -> p (a h) c", c=128)), [psv], [v_])
                    K.dma("sp", V1[vchunk0 + i].rearrange("p (h c) -> p h c", c=129), v_[:], v_, False)
            if ctab is not None:
                K.dma("sp", cs_[:, 0:ntok], ctab, cs_, True)
                K.dma("sp", sn_[:, 0:ntok], stab, sn_, True)
            for h in range(8):
                p1, p2 = ps1[h % 2], ps2[h % 2]
                for k in range(8):
                    K.op("pe", lambda e: e.matmul(out=p1[:, 0:ntok], lhsT=wb[:, k, col0 + h * 128:col0 + (h + 1) * 128], rhs=hT4[:, k, 0:ntok], start=(k == 0), stop=(k == 7)), [wb, hT4], [p1])
                kk = kb_[h % 2]
                if ctab is not None:
                    for k in range(8):
                        K.op("pe", lambda e: e.matmul(out=p2[:, 0:ntok], lhsT=wpb[:, k, col0 + h * 128:col0 + (h + 1) * 128], rhs=hT4[:, k, 0:ntok], start=(k == 0), stop=(k == 7)), [wpb, hT4], [p2])
                    a1, a2 = t1[h % 2], t2[h % 2]
                    K.op("dve", lambda e: e.tensor_tensor(out=a1[:, 0:ntok], in0=p1[:, 0:ntok], in1=cs_[:, 0:ntok], op=ALU.mult), [p1, cs_], [a1])
                    K.op("dve", lambda e: e.tensor_tensor(out=a2[:, 0:ntok], in0=p2[:, 0:ntok], in1=sn_[:, 0:ntok], op=ALU.mult), [p2, sn_], [a2])
                    K.op("pool", lambda e: e.tensor_tensor(out=kk[:, 0:ntok], in0=a1[:, 0:ntok], in1=a2[:, 0:ntok], op=ALU.add), [a1, a2], [kk])
                else:
                    K.op("act", lambda e: e.copy(out=kk[:, 0:ntok], in_=p1[:, 0:ntok]), [p1], [kk])
                K.dma("sp", dstT[h, :, tok0:tok0 + ntok], kk[:, 0:ntok], kk, False)

        block.A, block.B = Al, Bl
        for blk in range(SEQ // 512):
            tiles = [x2f[(blk * 4 + i) * 128:(blk * 4 + i + 1) * 128, :] for i in range(4)]
            block(tiles, 512, cosk[:, blk * 512:(blk + 1) * 512], sink[:, blk * 512:(blk + 1) * 512], 1024, KT1, blk * 512, blk * 4)
        for blk in range(HALF // 512):
            tiles = [x2m[(blk * 4 + i) * 128:(blk * 4 + i + 1) * 128, :] for i in range(4)]
            block(tiles, 512, cosq[:, blk * 512:(blk + 1) * 512], sinq[:, blk * 512:(blk + 1) * 512], 0, QT1, blk * 512, None)
        block.A, block.B = Ac, Bc
        tiles = [xc2[i * 128:(i + 1) * 128, :] for i in range(2)]
        block(tiles, 256, None, None, 1024, KT1, SEQ, SEQ // 128)
    if stop_after == "proj":
        return P, {}

    with K.phase():
        lt = K.sb("lam4", [128, 4, 64], F32)
        K.dma("sp", lt[:], lam4.partition_broadcast(128), lt, True)
        lp_ = K.sb("lamp", [128, 2, 64], F32)
        ls = K.sb("lams", [128, 2], F32)
        lam = K.sb("lam", [128, 1], F32)
        K.op("dve", lambda e: e.tensor_tensor(out=lp_[:, 0, :], in0=lt[:, 0, :], in1=lt[:, 1, :], op=ALU.mult), [lt], [lp_])
        K.op("dve", lambda e: e.tensor_tensor(out=lp_[:, 1, :], in0=lt[:, 2, :], in1=lt[:, 3, :], op=ALU.mult), [lt], [lp_])
        K.op("dve", lambda e: e.tensor_reduce(out=ls[:], in_=lp_[:], axis=AX.X, op=ALU.add), [lp_], [ls])
        K.op("act", lambda e: e.activation(out=ls[:], in_=ls[:], func=AF.Exp), [ls], [ls])
        K.op("dve", lambda e: e.tensor_tensor(out=lam[:], in0=ls[:, 0:1], in1=ls[:, 1:2], op=ALU.subtract), [ls], [lam])
        K.op("dve", lambda e: e.tensor_scalar(out=lam[:], in0=lam[:], scalar1=LAMBDA_INIT, scalar2=None, op0=ALU.add), [lam], [lam])
        gsc = K.sb("gsc", [128, 128], F32)
        K.dma("sp", gsc[:], subg.partition_broadcast(128), gsc, True)
        K.op("dve", lambda e: e.tensor_scalar(out=gsc[:], in0=gsc[:], scalar1=1.0 - LAMBDA_INIT, scalar2=None, op0=ALU.mult), [gsc], [gsc])
        KTh = K.sb("KTh", [128, SEQ + NCTX], BF16)
        Vh = K.sb("Vh", [128, NKC, 129], BF16)
        QTh = K.sb("QTh", [128, HALF], BF16)
        pst = [K.ps("pst%d" % i, [128, 512], F32) for i in range(2)]
        pso = [K.ps("pso%d" % i, [128, 512], F32) for i in range(4)]
        PT = [K.sb("PTd%d" % i, [128, 512], BF16) for i in range(2)]
        Oc = [K.sb("Oc%d" % i, [128, 4, 129], F32) for i in range(2)]
        rr = K.sb("rr", [128, 2, 4], F32)
        tt_ = K.sb("tto", [128, 128], F32)
        oo = K.sb("oo", [128, 128], F32)
        sq = K.sb("sqo", [128, 128], F32)
        ss = K.sb("sso", [128, 1], F32)
        rs = K.sb("rso", [128, 1], F32)
        ao = [K.sb("ao%d" % i, [128, 128], BF16) for i in range(2)]
        V1v = V1.rearrange("c p f -> p c f")
        nqb = lat_tiles // 4
        for h in range(8):
            K.dma("sp", KTh[:], KT1[h], KTh, True)
            K.dma("sp", Vh[:], V1v[:, :, h * 129:(h + 1) * 129], Vh, True)
            K.dma("sp", QTh[:], QT1[h], QTh, True)
            for qb in range(nqb):
                for comp in range(2):
                    c0 = comp * 64
                    for kc in range(NKC):
                        st_, pt_ = pst[kc % 2], PT[kc % 2]
                        K.op("pe", lambda e: e.matmul(out=st_[:], lhsT=KTh[c0:c0 + 64, kc * 128:(kc + 1) * 128], rhs=QTh[c0:c0 + 64, qb * 512:(qb + 1) * 512], start=True, stop=True), [KTh, QTh], [st_])
                        K.op("act", lambda e: e.activation(out=pt_[:], in_=st_[:], func=AF.Exp, scale=0.125), [st_], [pt_])
                        for qs in range(4):
                            K.op("pe", lambda e: e.matmul(out=pso[qs][:, 0:129], lhsT=pt_[:, qs * 128:(qs + 1) * 128], rhs=Vh[:, kc, :], start=(kc == 0), stop=(kc == NKC - 1)), [pt_, Vh], [pso[qs]])
                    for qs in range(4):
                        K.op("dve" if qs % 2 == 0 else "act",
                             (lambda e: e.tensor_copy(out=Oc[comp][:, qs, :], in_=pso[qs][:, 0:129])) if qs % 2 == 0 else
                             (lambda e: e.copy(out=Oc[comp][:, qs, :], in_=pso[qs][:, 0:129])), [pso[qs]], [Oc[comp]])
                K.op("dve", lambda e: e.reciprocal(out=rr[:, 0, :], in_=Oc[0][:, :, 128]), [Oc[0]], [rr])
                K.op("dve", lambda e: e.reciprocal(out=rr[:, 1, :], in_=Oc[1][:, :, 128]), [Oc[1]], [rr])
                K.op("dve", lambda e: e.tensor_scalar(out=rr[:, 1, :], in0=rr[:, 1, :], scalar1=lam[:], scalar2=None, op0=ALU.mult), [rr, lam], [rr])
                for qs in range(4):
                    K.op("dve", lambda e: e.tensor_scalar(out=tt_[:], in0=Oc[1][:, qs, 0:128], scalar1=rr[:, 1, qs:qs + 1], scalar2=None, op0=ALU.mult), [Oc[1], rr], [tt_])
                    K.op("dve", lambda e: e.scalar_tensor_tensor(out=oo[:], in0=Oc[0][:, qs, 0:128], scalar=rr[:, 0, qs:qs + 1], in1=tt_[:], op0=ALU.mult, op1=ALU.subtract), [Oc[0], rr, tt_], [oo])
                    K.op("act", lambda e: e.activation(out=sq[:], in_=oo[:], func=AF.Square, accum_out=ss[:]), [oo], [sq, ss])
                    K.op("dve", lambda e: e.tensor_scalar(out=rs[:], in0=ss[:], scalar1=1.0 / 128.0, scalar2=1e-6, op0=ALU.mult, op1=ALU.add), [ss], [rs])
                    K.op("act", lambda e: e.activation(out=rs[:], in_=rs[:], func=AF.Sqrt), [rs], [rs])
                    K.op("dve", lambda e: e.reciprocal(out=rs[:], in_=rs[:]), [rs], [rs])
                    a_ = ao[qs % 2]
                    K.op("dve", lambda e: e.scalar_tensor_tensor(out=a_[:], in0=oo[:], scalar=rs[:], in1=gsc[:], op0=ALU.mult, op1=ALU.mult), [oo, rs, gsc], [a_])
                    r0 = qb * 512 + qs * 128
                    K.dma("sp", AO[r0:r0 + 128, h * 128:(h + 1) * 128], a_[:], a_, False)
    if stop_after == "attn":
        return P, {}

    with K.phase():
        woutb = K.sb("woutb", [128, 8, D], BF16)
        P.load_w_bf16(woutb, wout, D, tag="o")
        gm = K.sb("gm", [128, D], F32)
        K.dma("sp", gm[:], MODL[:, 2 * D:3 * D], gm, True)
        tp = K.ps("tp3", [128, 8, 128], BF16)
        psy = K.ps("psy", [128, 2, 512], F32)
        mixs = [K.sb("mix%d" % i, [128, D], BF16) for i in range(2)]
        mixT = K.sb("mixT", [128, 8, 128], BF16)
        xr = K.sb("xr", [128, D], F32)
        x1 = K.sb("x1", [128, D], F32)
        for t in range(lat_tiles):
            mix = mixs[t % 2]
            K.dma("sp", mix[:], AO[t * 128:(t + 1) * 128, :], mix, True)
            P.transpose8(mix, tp, mixT)
            for hf_ in range(2):
                for k in range(8):
                    K.op("pe", lambda e: e.matmul(out=psy[:, hf_, :], lhsT=mixT[:, k, :], rhs=woutb[:, k, hf_ * 512:(hf_ + 1) * 512], start=(k == 0), stop=(k == 7)), [mixT, woutb], [psy])
            K.dma("sp", xr[:], x2m[t * 128:(t + 1) * 128, :], xr, True)
            K.op("dve", lambda e: e.tensor_tensor(out=x1[:], in0=psy[:].rearrange("p a b -> p (a b)"), in1=gm[:], op=ALU.mult), [psy, gm], [x1])
            K.op("pool", lambda e: e.tensor_tensor(out=x1[:], in0=x1[:], in1=xr[:], op=ALU.add), [x1, xr], [x1])
            K.dma("sp", X3[t * 128:(t + 1) * 128, :], x1[:], x1, False)
    if stop_after == "mix":
        return P, {}

    P.peer_convert(uT, pv, uTb, vb)
    srcs = [X3[t * 128:(t + 1) * 128, :] for t in range(lat_tiles)]
    dsts = [out[t * 128:(t + 1) * 128, :] for t in range(lat_tiles)]
    P.peer_layer(srcs, dsts, MODL, gffn, wq, skT, uTb, vb, "l", final_g=fng)
    return P, {}


def _rope_tables():
    n = np.arange(SEQ)
    row = (n // 64).astype(np.float64)
    col = (n % 64).astype(np.float64)
    inv = 10000.0 ** (-np.arange(16, dtype=np.float64) / 16.0)
    ang = np.stack([row[:, None] * inv, col[:, None] * inv], axis=1)
    cos = np.cos(ang.astype(np.float32)).astype(np.float32)
    sin = np.sin(ang.astype(np.float32)).astype(np.float32)
    C = np.zeros((64, SEQ), np.float32)
    S = np.zeros((64, SEQ), np.float32)
    for a in range(2):
        for j in range(2):
            C[a * 32 + j * 16:a * 32 + (j + 1) * 16, :] = cos[:, a, :].T
            S[a * 32 + j * 16:a * 32 + (j + 1) * 16, :] = (sin[:, a, :].T) * (-1.0 if j == 0 else 1.0)
    return np.concatenate([C, C], axis=0), np.concatenate([S, S], axis=0)


def _perm_cols():
    d = np.arange(64)
    a, j, i = d // 32, (d // 16) % 2, d % 16
    pd = a * 32 + (1 - j) * 16 + i
    f = np.arange(2048)
    return (f // 64) * 64 + pd[f % 64]


_ROPE = {}


def prep_B(inp, cid, x2, xc2):
    b, hb = cid // 2, cid % 2
    if "t" not in _ROPE:
        _ROPE["t"] = _rope_tables()
    C, S = _ROPE["t"]
    m = {"ident": np.eye(128, dtype=np.float32)}
    m["x2f"] = np.ascontiguousarray(x2[b])
    m["x2m"] = np.ascontiguousarray(x2[b, hb * HALF:(hb + 1) * HALF])
    m["xc2"] = np.ascontiguousarray(xc2[b])
    m["cpk"] = np.ascontiguousarray(inp["c"][b].reshape(8, 128).T)
    m["ccpk"] = np.ascontiguousarray(inp["c_ctx"].reshape(8, 128).T)
    m["wmod"] = np.ascontiguousarray(inp["w_mod"][1])
    m["bmod"] = np.ascontiguousarray(inp["b_mod"][1][None, :])
    m["gmix"] = np.ascontiguousarray(inp["norm_mix_g"][1])
    m["gffn"] = np.ascontiguousarray(inp["norm_ffn_g"][1])
    w = inp["odd_w_in"][0]
    m["win"] = np.ascontiguousarray(w)
    m["winp"] = np.ascontiguousarray(w[:, _perm_cols()])
    m["wout"] = np.ascontiguousarray(inp["odd_w_out"][0])
    m["cosk"] = C
    m["sink"] = S
    m["cosq"] = np.ascontiguousarray(C[:, hb * HALF:(hb + 1) * HALF])
    m["sinq"] = np.ascontiguousarray(S[:, hb * HALF:(hb + 1) * HALF])
    m["lam4"] = np.stack([inp["diff_lambda_q1"][0], inp["diff_lambda_k1"][0], inp["diff_lambda_q2"][0], inp["diff_lambda_k2"][0]]).astype(np.float32)
    m["subg"] = np.ascontiguousarray(inp["diff_subln_g"][0])
    m["wq"] = np.ascontiguousarray(inp["peer_w_q"][1])
    m["skT"] = np.ascontiguousarray(inp["peer_sub_keys"][1].reshape(16, 128, 128).transpose(0, 2, 1))
    m["uT"] = np.ascontiguousarray(inp["peer_u"][1].T)
    m["pv"] = np.ascontiguousarray(inp["peer_v"][1])
    m["fng"] = np.ascontiguousarray(inp["final_norm_g"])
    return m


_PROGS = {}


def _get_prog(which):
    if which not in _PROGS:
        P, _ = (build_A if which == "A" else build_B)()
        P.K.finish()
        _PROGS[which] = P
    return _PROGS[which]


def kernel(**inputs):
    inp = {k: np.asarray(v) for k, v in inputs.items()}
    n = 8
    PA_ = _get_prog("A")
    maps = [prep_A(inp, c) for c in range(n)]
    res = run_bass_kernel_spmd(PA_.nc, maps, core_ids=list(range(n)))
    del maps
    x2 = np.empty((4, SEQ, D), np.float32)
    xc2 = np.empty((4, NCTX, D), np.float32)
    for c in range(n):
        b, hb = c // 2, c % 2
        x2[b, hb * HALF:(hb + 1) * HALF] = np.asarray(res.results[c]["xo"])
        if hb == 0:
            xc2[b] = np.asarray(res.results[c]["xco"])
    PB_ = _get_prog("B")
    maps = [prep_B(inp, c, x2, xc2) for c in range(n)]
    res = run_bass_kernel_spmd(PB_.nc, maps, core_ids=list(range(n)))
    del maps
    out = np.empty((4, SEQ, D), np.float32)
    for c in range(n):
        b, hb = c // 2, c % 2
        out[b, hb * HALF:(hb + 1) * HALF] = np.asarray(res.results[c]["out"])
    return out
```
